# Optimizing a Trainium2 kernel written in Bass

```python
import math
import jax, jax.numpy as jnp
from jax import lax
import numpy as np

D_MODEL = 2048
BATCH = 2
SEQ = 16384
DEPTH = 2

F32 = jnp.float32
HG_HEADS = 4
HG_KDIM = 128
HG_VDIM = 128
HG_CHUNK = 64
RET_HEADS = 4
RET_KDIM = 64
RET_VDIM = 128
RET_CHUNK = 128
ROPE_BASE = 10000.0
ATT_HEADS = 8
ATT_HDIM = 128
DILATED_CONFIGS = ((128, 1), (512, 4), (2048, 16))
N_BUCKETS = 32
MAX_DISTANCE = 2048
D_FF = 5504
CONV_WIDTH = 3
EPS = 1e-6

HG_QK = HG_HEADS * HG_KDIM
HG_W = HG_HEADS * HG_VDIM
RET_QK = RET_HEADS * RET_KDIM
RET_W = RET_HEADS * RET_VDIM
ATT_W = ATT_HEADS * ATT_HDIM
MIX_W = HG_W + RET_W + ATT_W
IN_SIZES = (HG_QK, HG_QK, HG_W, HG_W, RET_QK, RET_QK, RET_W, RET_W, ATT_W, ATT_W, ATT_W)
IN_W = sum(IN_SIZES)

kernel_name = 'hybrid_hgrn2_retention_dilated_attn_convffn'


def rms_norm(x, g=None):
    xf = x.astype(F32)
    y = xf * lax.rsqrt(jnp.mean(xf * xf, axis=-1, keepdims=True) + EPS)
    if g is not None:
        y = y * g.astype(F32)
    return y.astype(x.dtype)


def hgrn2_mixer(q, f_logit, i, g, lower_bound):
    dtype = q.dtype
    B, S, _ = q.shape
    N = S // HG_CHUNK
    q = jax.nn.silu(q.astype(F32))
    xf = f_logit.astype(F32)
    lb = lower_bound.astype(F32)
    log_f = jnp.log(lb + (1.0 - lb) * jax.nn.sigmoid(xf))
    k = (1.0 - lb) * jax.nn.sigmoid(-xf)
    v = i.astype(F32)

    def to_chunks(t, d):
        return t.reshape(B, N, HG_CHUNK, HG_HEADS, d).transpose(1, 0, 3, 2, 4)

    causal = jnp.tril(jnp.ones((HG_CHUNK, HG_CHUNK), dtype=bool))

    def step(state, chunk):
        qc, kc, lfc, vc = chunk
        b = jnp.cumsum(lfc, axis=2)
        rel = jnp.where(causal[:, :, None], b[:, :, :, None, :] - b[:, :, None, :, :], -jnp.inf)
        scores = jnp.einsum('bhtc,bhsc,bhtsc->bhts', qc, kc, jnp.exp(rel))
        out = (jnp.einsum('bhts,bhsv->bhtv', scores, vc)
               + jnp.einsum('bhtc,bhcv->bhtv', qc * jnp.exp(b), state))
        b_last = b[:, :, -1, :]
        new_state = (jnp.exp(b_last)[..., None] * state
                     + jnp.einsum('bhsc,bhsv->bhcv', kc * jnp.exp(b_last[:, :, None, :] - b), vc))
        return new_state, out

    state0 = jnp.zeros((B, HG_HEADS, HG_KDIM, HG_VDIM), F32)
    _, o = lax.scan(step, state0, (to_chunks(q, HG_KDIM), to_chunks(k, HG_KDIM),
                                   to_chunks(log_f, HG_KDIM), to_chunks(v, HG_VDIM)))
    o = o.transpose(1, 0, 3, 2, 4).reshape(B, S, HG_HEADS, HG_VDIM)
    o = rms_norm(o).reshape(B, S, HG_W) * jax.nn.silu(g.astype(F32))
    return o.astype(dtype)


def apply_rope(t, cos, sin):
    half = t.shape[-1] // 2
    t1, t2 = t[..., :half], t[..., half:]
    c, s = cos[None, :, None, :], sin[None, :, None, :]
    return jnp.concatenate([t1 * c - t2 * s, t1 * s + t2 * c], axis=-1)


def retention_mixer(q, k, v, g, cos, sin):
    dtype = q.dtype
    B, S, _ = q.shape
    C = RET_CHUNK
    N = S // C
    q = apply_rope(q.astype(F32).reshape(B, S, RET_HEADS, RET_KDIM), cos, sin)
    k = apply_rope(k.astype(F32).reshape(B, S, RET_HEADS, RET_KDIM), cos, sin) * RET_KDIM ** -0.5
    v = v.astype(F32).reshape(B, S, RET_HEADS, RET_VDIM)
    log_gamma = jnp.log(1.0 - 2.0 ** (-5.0 - jnp.arange(RET_HEADS, dtype=F32)))
    qc = q.reshape(B, N, C, RET_HEADS, RET_KDIM)
    kc = k.reshape(B, N, C, RET_HEADS, RET_KDIM)
    vc = v.reshape(B, N, C, RET_HEADS, RET_VDIM)
    pos = jnp.arange(C)
    dist = pos[:, None] - pos[None, :]
    decay = jnp.where(dist >= 0, jnp.exp(log_gamma[:, None, None] * jnp.maximum(dist, 0)), 0.0)
    scores = jnp.einsum('bnthd,bnshd->bnhts', qc, kc) * decay
    intra = jnp.einsum('bnhts,bnshv->bnthv', scores, vc)
    to_end = jnp.exp(log_gamma[:, None] * (C - 1 - pos)[None, :])
    from_start = jnp.exp(log_gamma[:, None] * (pos + 1)[None, :])
    chunk_summary = jnp.einsum('bnshd,hs,bnshv->bnhdv', kc, to_end, vc)
    ci = jnp.arange(N)
    cd = ci[:, None] - ci[None, :] - 1
    chunk_decay = jnp.where(cd >= 0, jnp.exp(log_gamma[:, None, None] * C * jnp.maximum(cd, 0)), 0.0)
    state_in = jnp.einsum('hnm,bmhdv->bnhdv', chunk_decay, chunk_summary)
    inter = jnp.einsum('bnthd,ht,bnhdv->bnthv', qc, from_start, state_in)
    o = rms_norm((intra + inter).reshape(B, S, RET_HEADS, RET_VDIM)).reshape(B, S, RET_W)
    return (jax.nn.silu(g.astype(F32)) * o).astype(dtype)


def t5_bucket(distance):
    max_exact = N_BUCKETS // 2
    n = jnp.maximum(distance, 1).astype(F32)
    large = max_exact + (jnp.log(n / max_exact) / math.log(MAX_DISTANCE / max_exact)
                         * (N_BUCKETS - max_exact)).astype(jnp.int32)
    large = jnp.minimum(large, N_BUCKETS - 1)
    return jnp.where(distance < max_exact, distance, large)


def dilated_branch(q, k, v, window, dilation, rel_bias):
    B, S, H, hd = q.shape
    blk = window // dilation
    L = S // dilation

    def by_residue(t):
        return t.reshape(B, L, dilation, H, hd).transpose(0, 2, 3, 1, 4)

    qs, ks, vs = by_residue(q), by_residue(k), by_residue(v)
    nb = -(-L // blk)
    Lp = nb * blk
    qb = jnp.pad(qs, ((0, 0), (0, 0), (0, 0), (0, Lp - L), (0, 0))).reshape(B, dilation, H, nb, blk, hd)
    kv_pad = ((0, 0), (0, 0), (0, 0), (blk, Lp - L), (0, 0))
    kp = jnp.pad(ks, kv_pad).reshape(B, dilation, H, nb + 1, blk, hd)
    vp = jnp.pad(vs, kv_pad).reshape(B, dilation, H, nb + 1, blk, hd)
    keys = jnp.concatenate([kp[:, :, :, :-1], kp[:, :, :, 1:]], axis=-2)
    vals = jnp.concatenate([vp[:, :, :, :-1], vp[:, :, :, 1:]], axis=-2)
    a = jnp.arange(blk)[:, None]
    kk = jnp.arange(2 * blk)[None, :]
    j = a + blk - kk
    key_idx = jnp.arange(nb)[:, None, None] * blk + kk[None] - blk
    valid = ((j >= 0) & (j <= blk))[None] & (key_idx >= 0)
    bias = rel_bias.astype(F32)[t5_bucket(jnp.clip(j, 0, blk) * dilation)].transpose(2, 0, 1)
    logits = jnp.einsum('bdhnqe,bdhnke->bdhnqk', qb, keys) + bias[:, None]
    logits = jnp.where(valid, logits, -jnp.inf)
    m = jnp.max(logits, axis=-1)
    p = jnp.exp(logits - m[..., None])
    s = jnp.sum(p, axis=-1)
    o = jnp.einsum('bdhnqk,bdhnke->bdhnqe', p, vals) / s[..., None]
    o = o.reshape(B, dilation, H, Lp, hd)[:, :, :, :L].transpose(0, 3, 1, 2, 4).reshape(B, S, H, hd)
    m = m.reshape(B, dilation, H, Lp)[..., :L].transpose(0, 3, 1, 2).reshape(B, S, H)
    s = s.reshape(B, dilation, H, Lp)[..., :L].transpose(0, 3, 1, 2).reshape(B, S, H)
    return o, m, s


def dilated_attention(q, k, v, rel_bias):
    dtype = q.dtype
    B, S, _ = q.shape
    q = q.astype(F32).reshape(B, S, ATT_HEADS, ATT_HDIM) * ATT_HDIM ** -0.5
    k = k.astype(F32).reshape(B, S, ATT_HEADS, ATT_HDIM)
    v = v.astype(F32).reshape(B, S, ATT_HEADS, ATT_HDIM)
    outs, maxes, sums = [], [], []
    for window, dilation in DILATED_CONFIGS:
        o, m, s = dilated_branch(q, k, v, window, dilation, rel_bias)
        outs.append(o)
        maxes.append(m)
        sums.append(s)
    M = jnp.stack(maxes)
    wts = jnp.stack(sums) * jnp.exp(M - jnp.max(M, axis=0))
    o = jnp.einsum('cbsh,cbshe->bshe', wts, jnp.stack(outs)) / jnp.sum(wts, axis=0)[..., None]
    return o.reshape(B, S, ATT_W).astype(dtype)


def causal_dwconv(h, w, b):
    K = w.shape[0]
    out = lax.conv_general_dilated(h, w[:, None, :], window_strides=(1,), padding=[(K - 1, 0)],
                                   dimension_numbers=('NWC', 'WIO', 'NWC'),
                                   feature_group_count=h.shape[-1])
    return out + b


def setup_inputs(seed: int = 0) -> dict:
    key = jax.random.key(seed)
    ks = jax.random.split(key, 13)

    def normal(k, shape, scale):
        return jax.random.normal(k, shape, F32) * scale

    return {
        'x': normal(ks[0], (BATCH, SEQ, D_MODEL), 1.0),
        'norm_mix': 1.0 + normal(ks[1], (DEPTH, D_MODEL), 0.02),
        'w_in': normal(ks[2], (DEPTH, D_MODEL, IN_W), D_MODEL ** -0.5),
        'hg_lower_bound': normal(ks[3], (DEPTH, HG_QK), 0.5),
        'w_out': normal(ks[4], (DEPTH, MIX_W, D_MODEL), MIX_W ** -0.5),
        'norm_ffn': 1.0 + normal(ks[5], (DEPTH, D_MODEL), 0.02),
        'w_gate': normal(ks[6], (DEPTH, D_MODEL, D_FF), D_MODEL ** -0.5),
        'conv_w': normal(ks[7], (DEPTH, CONV_WIDTH, D_FF), CONV_WIDTH ** -0.5),
        'conv_b': normal(ks[8], (DEPTH, D_FF), 0.02),
        'w_up': normal(ks[9], (DEPTH, D_MODEL, D_FF), D_MODEL ** -0.5),
        'w_down': normal(ks[10], (DEPTH, D_FF, D_MODEL), D_FF ** -0.5),
        'rel_bias': normal(ks[11], (N_BUCKETS, ATT_HEADS), 0.2),
        'norm_final': 1.0 + normal(ks[12], (D_MODEL,), 0.02),
    }


def reference(x, norm_mix, w_in, hg_lower_bound, w_out, norm_ffn, w_gate, conv_w, conv_b,
              w_up, w_down, rel_bias, norm_final):
    S = x.shape[1]
    inv_freq = ROPE_BASE ** (-jnp.arange(0, RET_KDIM, 2, dtype=F32) / RET_KDIM)
    ang = jnp.arange(S, dtype=F32)[:, None] * inv_freq[None, :]
    cos, sin = jnp.cos(ang), jnp.sin(ang)
    lb_all = jnp.cumsum(jax.nn.softmax(hg_lower_bound.astype(F32), axis=0), axis=0)
    lb_all = lb_all - lb_all[0]
    split_at = [int(o) for o in np.cumsum(IN_SIZES)[:-1]]
    for l in range(DEPTH):
        h = rms_norm(x, norm_mix[l])
        proj = h @ w_in[l]
        hq, hf, hi, hg, rq, rk, rv, rg, aq, ak, av = jnp.split(proj, split_at, axis=-1)
        mixed = jnp.concatenate([
            hgrn2_mixer(hq, hf, hi, hg, lb_all[l]),
            retention_mixer(rq, rk, rv, rg, cos, sin),
            dilated_attention(aq, ak, av, rel_bias),
        ], axis=-1)
        x = x + mixed @ w_out[l]
        u = rms_norm(x, norm_ffn[l])
        gate = causal_dwconv(u @ w_gate[l], conv_w[l], conv_b[l])
        x = x + (jax.nn.silu(gate) * (u @ w_up[l])) @ w_down[l]
    return rms_norm(x, norm_final)
```

```python
import os
import math
import numpy as np
import ml_dtypes
from concourse.bass_utils import run_bass_kernel_spmd

from contextlib import ExitStack
import concourse.bass as bass
import concourse.mybir as mybir

F32 = mybir.dt.float32
BF16 = mybir.dt.bfloat16
ALU = mybir.AluOpType
AF = mybir.ActivationFunctionType
AX = mybir.AxisListType

STREAMS = ["sync", "scalar", "vector", "gpsimd", "tensor"]
DMA_K = 8
SEM_LIMIT = 30000


class Buf:
    __slots__ = ("name", "last_write", "readers")

    def __init__(self, name=""):
        self.name = name
        self.last_write = None
        self.readers = []


class Op:
    __slots__ = ("stream", "vq", "fn", "deps", "idx", "signal", "semref", "dma", "dma_slot")

    def __init__(self, stream, vq, fn, dma):
        self.stream = stream
        self.vq = vq
        self.fn = fn
        self.deps = []
        self.signal = False
        self.semref = None
        self.dma = dma
        self.dma_slot = None


class Prog:
    def __init__(self, nc, same_engine_sync=True):
        self.nc = nc
        self.es = ExitStack()
        self.ops = {s: [] for s in STREAMS}
        self.same_engine_sync = same_engine_sync
        self.dma_count = {}
        self.all_ops = []

    def sb(self, name, shape, dtype):
        return self.es.enter_context(self.nc.sbuf_tensor(name, list(shape), dtype))

    def ps(self, name, shape, dtype):
        return self.es.enter_context(self.nc.psum_tensor(name, list(shape), dtype))

    def op(self, stream, fn, reads=(), writes=(), dma=False):
        vq = stream + "_dma" if dma else stream
        o = Op(stream, vq, fn, dma)
        deps = set()
        for b in reads:
            if b.last_write is not None:
                deps.add(b.last_write)
        for b in writes:
            if b.last_write is not None:
                deps.add(b.last_write)
            for r in b.readers:
                deps.add(r)
        for d in deps:
            if d is o:
                continue
            if d.vq == vq and not dma:
                if vq == "tensor" or not self.same_engine_sync:
                    continue
            o.deps.append(d)
            d.signal = True
        if dma:
            n = self.dma_count.get(vq, 0)
            self.dma_count[vq] = n + 1
            o.dma_slot = n
            o.signal = True
        for b in reads:
            b.readers.append(o)
        for b in writes:
            b.last_write = o
            b.readers = []
        self.ops[stream].append(o)
        self.all_ops.append(o)
        return o

    def dma(self, stream, out, in_, reads=(), writes=(), **kw):
        return self.op(stream, lambda e: e.dma_start(out=out, in_=in_, **kw), reads, writes, dma=True)

    def emit(self):
        nc = self.nc
        es = self.es
        sems = {}

        def get_sem(key):
            if key not in sems:
                sems[key] = es.enter_context(nc.semaphore("s_" + key.replace(":", "_")))
            return sems[key]

        cnt = {}
        dma_last = {}
        for o in self.all_ops:
            if o.dma:
                slot = o.dma_slot % DMA_K
                rnd = o.dma_slot // DMA_K
                o.semref = ("%s:%d" % (o.vq, slot), 16 * (rnd + 1))
            elif o.signal:
                c = cnt.get(o.vq, 0) + 1
                cnt[o.vq] = c
                chunk = (c - 1) // SEM_LIMIT
                o.semref = ("%s:c%d" % (o.vq, chunk), c - chunk * SEM_LIMIT)
        for o in self.all_ops:
            if o.semref is not None:
                get_sem(o.semref[0])

        block = es.enter_context(nc.Block())
        stats = {}

        def run_stream(stream, eng):
            waited = {}
            nw = 0
            for o in self.ops[stream]:
                need = {}
                for d in o.deps:
                    k, v = d.semref
                    if v > need.get(k, 0):
                        need[k] = v
                if o.dma and o.dma_slot >= DMA_K:
                    k = "%s:%d" % (o.vq, o.dma_slot % DMA_K)
                    v = 16 * (o.dma_slot // DMA_K)
                    if v > need.get(k, 0):
                        need[k] = v
                for k, v in need.items():
                    if waited.get(k, 0) >= v:
                        continue
                    waited[k] = v
                    eng.wait_ge(sems[k], v)
                    nw += 1
                ins = o.fn(eng)
                if o.semref is not None:
                    ins.then_inc(sems[o.semref[0]], 16 if o.dma else 1)
            vq = stream + "_dma"
            n = self.dma_count.get(vq, 0)
            for slot in range(min(n, DMA_K)):
                last = ((n - 1 - slot) // DMA_K) * DMA_K + slot
                k = "%s:%d" % (vq, slot)
                v = 16 * (last // DMA_K + 1)
                if waited.get(k, 0) < v:
                    eng.wait_ge(sems[k], v)
            stats[stream] = (len(self.ops[stream]), nw)

        @block.sync
        def _(e):
            run_stream("sync", e)

        @block.scalar
        def _(e):
            run_stream("scalar", e)

        @block.vector
        def _(e):
            run_stream("vector", e)

        @block.gpsimd
        def _(e):
            run_stream("gpsimd", e)

        @block.tensor
        def _(e):
            run_stream("tensor", e)

        self.stats = stats
        es.close()
        return nc


D = 2048
KT = D // 128
FF = 5504
NJ = FF // 128
EPS = 1e-6


class NormCtx:
    def __init__(self, pg, tag, npt=2):
        self.pg = pg
        self.gain = pg.sb(tag + "_gain", [128, D], F32)
        self.gain_b = Buf("gain")
        self.ident = pg.sb(tag + "_ident", [128, 128], BF16)
        self.identf = pg.sb(tag + "_identf", [128, 128], F32)
        self.ident_b = Buf("ident")
        self.junk = pg.sb(tag + "_junk", [128, D], BF16)
        self.junk_b = Buf("junk")
        self.ss = [pg.sb(tag + "_ss%d" % i, [128, 1], F32) for i in range(2)]
        self.ss_b = [Buf() for i in range(2)]
        self.rs = [pg.sb(tag + "_rs%d" % i, [128, 1], F32) for i in range(2)]
        self.rs_b = [Buf() for i in range(2)]
        self.hb = [pg.sb(tag + "_h%d" % i, [128, D], BF16) for i in range(2)]
        self.hb_b = [Buf() for i in range(2)]
        self.pT = [pg.ps(tag + "_pT%d" % i, [128, D], BF16) for i in range(npt)]
        self.pT_b = [Buf() for i in range(npt)]
        self.npt = npt
        self.epsb = pg.sb(tag + "_eps", [128, 1], F32)
        self.eps_b = Buf("eps")
        self.n = 0
        pg.op("vector", lambda e: e.memset(self.epsb[:], EPS), writes=[self.eps_b])
        pg.op("gpsimd", lambda e: e.memset(self.identf[:], 0.0), writes=[self.ident_b])
        pg.op("gpsimd", lambda e: e.affine_select(out=self.identf[:], in_=self.identf[:], pattern=[[-1, 128]],
                                                  compare_op=ALU.not_equal, fill=1.0, base=0,
                                                  channel_multiplier=1),
              reads=[self.ident_b], writes=[self.ident_b])
        pg.op("vector", lambda e: e.tensor_copy(out=self.ident[:], in_=self.identf[:]),
              reads=[self.ident_b], writes=[self.ident_b])

    def load_gain(self, g_ap):
        self.pg.dma("sync", self.gain[:], g_ap.partition_broadcast(128), writes=[self.gain_b])

    def tile(self, xt, xt_b, dst, dst_b):
        pg = self.pg
        p = self.n % 2
        q = self.n % self.npt
        self.n += 1
        ss, rs, hb, pT, junk = self.ss[p], self.rs[p], self.hb[p], self.pT[q], self.junk
        gain, ident, epsb = self.gain, self.ident, self.epsb
        pg.op("scalar", lambda e: e.activation(out=junk[:], in_=xt[:], func=AF.Square, accum_out=ss[:]),
              reads=[xt_b], writes=[self.junk_b, self.ss_b[p]])
        pg.op("scalar", lambda e: e.activation(out=rs[:], in_=ss[:], func=AF.Ln, scale=1.0 / D, bias=epsb[:]),
              reads=[self.ss_b[p], self.eps_b], writes=[self.rs_b[p]])
        pg.op("scalar", lambda e: e.activation(out=rs[:], in_=rs[:], func=AF.Exp, scale=-0.5),
              reads=[self.rs_b[p]], writes=[self.rs_b[p]])
        pg.op("vector", lambda e: e.scalar_tensor_tensor(out=hb[:], in0=xt[:], scalar=rs[:], in1=gain[:],
                                                         op0=ALU.mult, op1=ALU.mult),
              reads=[xt_b, self.rs_b[p], self.gain_b], writes=[self.hb_b[p]])
        for k in range(KT):
            pg.op("tensor", lambda e, k=k: e.transpose(out=pT[:, k * 128:(k + 1) * 128],
                                                        in_=hb[:, k * 128:(k + 1) * 128], identity=ident[:]),
                  reads=[self.hb_b[p], self.ident_b], writes=[self.pT_b[q]])
        pg.op("scalar", lambda e: e.copy(out=dst, in_=pT[:].rearrange("p (k t) -> p k t", k=KT)),
              reads=[self.pT_b[q]], writes=[dst_b])


def phase_A(pg, x_ap, g_ap, hT_ap, T, tag="A"):
    G = 4
    nctx = NormCtx(pg, tag)
    nctx.load_gain(g_ap)
    xt = [pg.sb(tag + "_xt%d" % i, [128, D], F32) for i in range(2)]
    xt_b = [Buf() for i in range(2)]
    hT = [pg.sb(tag + "_hT%d" % i, [128, KT, 128 * G], BF16) for i in range(2)]
    hT_b = [Buf() for i in range(2)]
    out_b = Buf("hT_dram")
    for g in range(T // (128 * G)):
        hTt, hTb = hT[g % 2], hT_b[g % 2]
        for j in range(G):
            i = g * G + j
            p = i % 2
            pg.dma("sync", xt[p][:], x_ap[i * 128:(i + 1) * 128, :], writes=[xt_b[p]])
            nctx.tile(xt[p], xt_b[p], hTt[:, :, j * 128:(j + 1) * 128], hTb)
        pg.dma("sync", hT_ap[:, :, g * 128 * G:(g + 1) * 128 * G].rearrange("k p t -> p k t"), hTt[:],
               reads=[hTb], writes=[out_b])
    return out_b


def phase_C1(pg, mT_ap, x_ap, wout_ap, g_ap, x1_ap, uT_ap, T, tag="C1"):
    G = 4
    nctx = NormCtx(pg, tag, npt=2)
    nctx.load_gain(g_ap)
    wsb = pg.sb(tag + "_w", [128, KT, D], BF16)
    w_b = Buf("wout")
    wv = wout_ap.rearrange("(k p) d -> p k d", p=128)
    for q in range(4):
        pg.dma("sync", wsb[:, q * 4:(q + 1) * 4, :], wv[:, q * 4:(q + 1) * 4, :], writes=[w_b])
    xt = [pg.sb(tag + "_xt%d" % i, [128, D], F32) for i in range(2)]
    xt_b = [Buf() for i in range(2)]
    mT = [pg.sb(tag + "_mT%d" % i, [128, KT, 128 * G], BF16) for i in range(2)]
    mT_b = [Buf() for i in range(2)]
    uT = [pg.sb(tag + "_uT%d" % i, [128, KT, 128 * G], BF16) for i in range(2)]
    uT_b = [Buf() for i in range(2)]
    ps = [pg.ps(tag + "_ps%d" % i, [128, 512], F32) for i in range(4)]
    ps_b = [Buf() for i in range(4)]
    x1_dram_b = Buf("x1_dram")
    uT_dram_b = Buf("uT_dram")
    for g in range(T // (128 * G)):
        mTt, mTb = mT[g % 2], mT_b[g % 2]
        uTt, uTb = uT[g % 2], uT_b[g % 2]
        pg.dma("sync", mTt[:], mT_ap[:, :, g * 128 * G:(g + 1) * 128 * G].rearrange("k p t -> p k t"),
               writes=[mTb])
        for j in range(G):
            i = g * G + j
            p = i % 2
            pg.dma("sync", xt[p][:], x_ap[i * 128:(i + 1) * 128, :], writes=[xt_b[p]])
            for c in range(4):
                for k in range(KT):
                    pg.op("tensor", lambda e, c=c, k=k, j=j, mTt=mTt: e.matmul(
                        ps[c][:], lhsT=mTt[:, k, j * 128:(j + 1) * 128], rhs=wsb[:, k, c * 512:(c + 1) * 512],
                        start=(k == 0), stop=(k == KT - 1)),
                        reads=[mTb, w_b], writes=[ps_b[c]])
                pg.op("vector", lambda e, c=c, p=p: e.tensor_tensor(
                    out=xt[p][:, c * 512:(c + 1) * 512], in0=ps[c][:], in1=xt[p][:, c * 512:(c + 1) * 512],
                    op=ALU.add), reads=[ps_b[c], xt_b[p]], writes=[xt_b[p]])
            pg.dma("sync", x1_ap[i * 128:(i + 1) * 128, :], xt[p][:], reads=[xt_b[p]], writes=[x1_dram_b])
            nctx.tile(xt[p], xt_b[p], uTt[:, :, j * 128:(j + 1) * 128], uTb)
        pg.dma("sync", uT_ap[:, :, g * 128 * G:(g + 1) * 128 * G].rearrange("k p t -> p k t"), uTt[:],
               reads=[uTb], writes=[uT_dram_b])
    return x1_dram_b, uT_dram_b


def phase_C2(pg, uT_ap, uTh_ap, x1_ap, wg_ap, wu_ap, wd_ap, cw_ap, cb_ap, x2_ap, T, tag="C2"):
    nc = pg.nc
    TB = 512
    nblk = T // TB
    cw = pg.sb(tag + "_cw", [128, 4, NJ], F32)
    cw_b = Buf("cw")
    for k in range(3):
        pg.dma("sync", cw[:, k, :], cw_ap[k, :].rearrange("(j p) -> p j", p=128), writes=[cw_b],
               allow_slow_non_contiguous=True)
    pg.dma("sync", cw[:, 3, :], cb_ap.rearrange("(j p) -> p j", p=128), writes=[cw_b],
           allow_slow_non_contiguous=True)
    carry = pg.sb(tag + "_carry", [128, NJ, 2], F32)
    carry_b = Buf("carry")
    uT = [pg.sb(tag + "_uT%d" % i, [128, KT, TB], BF16) for i in range(2)]
    uT_b = [Buf() for i in range(2)]
    uTh = pg.sb(tag + "_uTh", [128, KT, 2], BF16)
    uTh_b = Buf()
    WG = 4
    wg = [pg.sb(tag + "_wg%d" % i, [128, KT, 128 * WG], BF16) for i in range(2)]
    wg_b = [Buf() for i in range(2)]
    wu = [pg.sb(tag + "_wu%d" % i, [128, KT, 128 * WG], BF16) for i in range(2)]
    wu_b = [Buf() for i in range(2)]
    actT = pg.sb(tag + "_actT", [128, NJ, TB], BF16)
    actT_b = [Buf() for j in range(NJ)]
    WDJ = 8
    wd = [pg.sb(tag + "_wd%d" % i, [128, WDJ, 512], BF16) for i in range(2)]
    wd_b = [Buf() for i in range(2)]
    x1c = [[pg.sb(tag + "_x1c%d_%d" % (i, t), [128, 512], F32) for t in range(4)] for i in range(2)]
    x1c_b = [[Buf() for t in range(4)] for i in range(2)]
    gs = [pg.sb(tag + "_gs%d" % i, [128, TB + 2], F32) for i in range(2)]
    gs_b = [Buf() for i in range(2)]
    t1 = [pg.sb(tag + "_t1%d" % i, [128, TB], F32) for i in range(2)]
    t1_b = [Buf() for i in range(2)]
    sl = [pg.sb(tag + "_sl%d" % i, [128, TB], F32) for i in range(2)]
    sl_b = [Buf() for i in range(2)]
    psg = [pg.ps(tag + "_psg%d" % i, [128, 512], F32) for i in range(2)]
    psg_b = [Buf() for i in range(2)]
    psu = [pg.ps(tag + "_psu%d" % i, [128, 512], F32) for i in range(2)]
    psu_b = [Buf() for i in range(2)]
    psd = [pg.ps(tag + "_psd%d" % i, [128, 512], F32) for i in range(4)]
    psd_b = [Buf() for i in range(4)]
    x2_dram_b = Buf("x2_dram")

    wgv = wg_ap.rearrange("(k p) f -> p k f", p=128)
    wuv = wu_ap.rearrange("(k p) f -> p k f", p=128)
    wdv = wd_ap.rearrange("(j p) d -> p j d", p=128)
    groups = [(a, min(a + WG, NJ)) for a in range(0, NJ, WG)]
    dgroups = [(a, min(a + WDJ, NJ)) for a in range(0, NJ, WDJ)]

    pg.dma("sync", uTh[:], uTh_ap.rearrange("k p t -> p k t"), writes=[uTh_b])
    wcnt = 0
    dcnt = 0
    jc = 0
    for b in range(nblk):
        uTt, uTb = uT[b % 2], uT_b[b % 2]
        pg.dma("sync", uTt[:], uT_ap[:, :, b * TB:(b + 1) * TB].rearrange("k p t -> p k t"), writes=[uTb])
        for (ja, jb) in groups:
            wp = wcnt % 2
            wcnt += 1
            nf = (jb - ja) * 128
            pg.dma("sync", wg[wp][:, :, 0:nf], wgv[:, :, ja * 128:jb * 128], writes=[wg_b[wp]])
            pg.dma("sync", wu[wp][:, :, 0:nf], wuv[:, :, ja * 128:jb * 128], writes=[wu_b[wp]])
            for j in range(ja, jb):
                q = jc % 2
                jc += 1
                jl = j - ja
                if b == 0:
                    for k in range(KT):
                        pg.op("tensor", lambda e, k=k, jl=jl, j=j, wp=wp: e.matmul(
                            psd[0][:, 2 * j:2 * j + 2], lhsT=wg[wp][:, k, jl * 128:(jl + 1) * 128],
                            rhs=uTh[:, k, :], start=(k == 0), stop=(k == KT - 1)),
                            reads=[wg_b[wp], uTh_b], writes=[psd_b[0]])
                    pg.op("scalar", lambda e, j=j: e.copy(out=carry[:, j, :], in_=psd[0][:, 2 * j:2 * j + 2]),
                          reads=[psd_b[0]], writes=[carry_b])
                for k in range(KT):
                    pg.op("tensor", lambda e, k=k, jl=jl, wp=wp, q=q, uTt=uTt: e.matmul(
                        psg[q][:], lhsT=wg[wp][:, k, jl * 128:(jl + 1) * 128], rhs=uTt[:, k, :],
                        start=(k == 0), stop=(k == KT - 1)),
                        reads=[wg_b[wp], uTb], writes=[psg_b[q]])
                for k in range(KT):
                    pg.op("tensor", lambda e, k=k, jl=jl, wp=wp, q=q, uTt=uTt: e.matmul(
                        psu[q][:], lhsT=wu[wp][:, k, jl * 128:(jl + 1) * 128], rhs=uTt[:, k, :],
                        start=(k == 0), stop=(k == KT - 1)),
                        reads=[wu_b[wp], uTb], writes=[psu_b[q]])
                pg.op("gpsimd", lambda e, j=j, q=q: e.tensor_copy(out=gs[q][:, 0:2], in_=carry[:, j, :]),
                      reads=[carry_b], writes=[gs_b[q]])
                pg.op("scalar", lambda e, q=q: e.copy(out=gs[q][:, 2:TB + 2], in_=psg[q][:]),
                      reads=[psg_b[q]], writes=[gs_b[q]])
                pg.op("gpsimd", lambda e, j=j, q=q: e.tensor_copy(out=carry[:, j, :], in_=gs[q][:, TB:TB + 2]),
                      reads=[gs_b[q]], writes=[carry_b])
                pg.op("vector", lambda e, j=j, q=q: e.tensor_scalar(
                    out=t1[q][:], in0=gs[q][:, 2:TB + 2], scalar1=cw[:, 2, j:j + 1], scalar2=cw[:, 3, j:j + 1],
                    op0=ALU.mult, op1=ALU.add), reads=[gs_b[q], cw_b], writes=[t1_b[q]])
                pg.op("vector", lambda e, j=j, q=q: e.scalar_tensor_tensor(
                    out=t1[q][:], in0=gs[q][:, 1:TB + 1], scalar=cw[:, 1, j:j + 1], in1=t1[q][:],
                    op0=ALU.mult, op1=ALU.add), reads=[gs_b[q], cw_b, t1_b[q]], writes=[t1_b[q]])
                pg.op("vector", lambda e, j=j, q=q: e.scalar_tensor_tensor(
                    out=t1[q][:], in0=gs[q][:, 0:TB], scalar=cw[:, 0, j:j + 1], in1=t1[q][:],
                    op0=ALU.mult, op1=ALU.add), reads=[gs_b[q], cw_b, t1_b[q]], writes=[t1_b[q]])
                pg.op("scalar", lambda e, q=q: e.activation(out=sl[q][:], in_=t1[q][:], func=AF.Silu),
                      reads=[t1_b[q]], writes=[sl_b[q]])
                pg.op("vector", lambda e, j=j, q=q: e.tensor_tensor(
                    out=actT[:, j, :], in0=sl[q][:], in1=psu[q][:], op=ALU.mult),
                    reads=[sl_b[q], psu_b[q]], writes=[actT_b[j]])
        for c in range(4):
            xp = (b * 4 + c) % 2
            for t in range(4):
                pg.dma("sync", x1c[xp][t][:],
                       x1_ap[b * TB + t * 128:b * TB + (t + 1) * 128, c * 512:(c + 1) * 512],
                       writes=[x1c_b[xp][t]])
            for (ja, jb) in dgroups:
                dp = dcnt % 2
                dcnt += 1
                pg.dma("sync", wd[dp][:, 0:jb - ja, :], wdv[:, ja:jb, c * 512:(c + 1) * 512], writes=[wd_b[dp]])
                for j in range(ja, jb):
                    for t in range(4):
                        pg.op("tensor", lambda e, j=j, t=t, dp=dp, ja=ja: e.matmul(
                            psd[t][:], lhsT=actT[:, j, t * 128:(t + 1) * 128], rhs=wd[dp][:, j - ja, :],
                            start=(j == 0), stop=(j == NJ - 1)),
                            reads=[actT_b[j], wd_b[dp]], writes=[psd_b[t]])
            for t in range(4):
                pg.op("vector", lambda e, t=t, xp=xp: e.tensor_tensor(
                    out=x1c[xp][t][:], in0=psd[t][:], in1=x1c[xp][t][:], op=ALU.add),
                    reads=[psd_b[t], x1c_b[xp][t]], writes=[x1c_b[xp][t]])
                pg.dma("sync", x2_ap[b * TB + t * 128:b * TB + (t + 1) * 128, c * 512:(c + 1) * 512],
                       x1c[xp][t][:], reads=[x1c_b[xp][t]], writes=[x2_dram_b])
    return x2_dram_b


def phase_W(pg, pairs, tag="W"):
    b = Buf("wconv")
    for (s, d) in pairs:
        rows = s.shape[0]
        step = 256
        for r in range(0, rows, step):
            r2 = min(rows, r + step)
            pg.dma("gpsimd", d[r:r2, :], s[r:r2, :], writes=[Buf()])
    return b


TMLEVEL = int(os.environ.get('TMLEVEL', '3'))
GP = os.environ.get('GPENG', 'vector')

NFM = 10
NTM = 384
TB = 512
SBS = 2048
RING = 2 * SBS
NEG = -30000.0


def phase_B(pg, hT_ap, wfm_ap, wtm_ap, rtab_ap, bmt_ap, hgb_ap, cst_ap, mixT_ap, S, layer, tag="B", stages=("stab", "tm", "hgrn", "ret", "att")):
    nc = pg.nc
    nblk = S // TB

    def sbt(name, shape, dt):
        return pg.sb(tag + "_" + name, shape, dt), Buf(name)

    ident_f, ident_b = sbt("identf", [128, 128], F32)
    ident, _ = sbt("ident", [128, 128], BF16)
    ones_bf, ones_b = sbt("ones", [128, 128], BF16)
    cmask, cmask_b = sbt("cmask", [128, 128], F32)
    segm, segm_b = sbt("segm", [128, TB], F32)
    epsb, eps_b = sbt("eps", [128, 1], F32)
    pg.op("gpsimd", lambda e: e.memset(ident_f[:], 0.0), writes=[ident_b])
    pg.op("gpsimd", lambda e: e.affine_select(out=ident_f[:], in_=ident_f[:], pattern=[[-1, 128]],
                                              compare_op=ALU.not_equal, fill=1.0, base=0, channel_multiplier=1),
          reads=[ident_b], writes=[ident_b])
    pg.op("vector", lambda e: e.tensor_copy(out=ident[:], in_=ident_f[:]), reads=[ident_b], writes=[ident_b])
    pg.op("vector", lambda e: e.memset(ones_bf[:], 1.0), writes=[ones_b])
    pg.op("gpsimd", lambda e: e.memset(cmask[:], 1.0), writes=[cmask_b])
    pg.op("gpsimd", lambda e: e.affine_select(out=cmask[:], in_=cmask[:], pattern=[[1, 128]],
                                              compare_op=ALU.is_ge, fill=0.0, base=0, channel_multiplier=-1),
          reads=[cmask_b], writes=[cmask_b])
    pg.op("vector", lambda e: e.memset(segm[:], 1.0), writes=[segm_b])
    for i in range(4):
        pg.op("vector", lambda e, i=i: e.memset(segm[:, i * 128:i * 128 + 1], 0.0), reads=[segm_b], writes=[segm_b])
    pg.op("vector", lambda e: e.memset(epsb[:], EPS), writes=[eps_b])

    wfm, wfm_b = sbt("wfm", [128, KT, NFM * 128], BF16)
    wtm, wtm_b = sbt("wtm", [128, KT, NTM], BF16)
    wfv = wfm_ap.rearrange("(k p) f -> p k f", p=128)
    for q in range(4):
        pg.dma("sync", wfm[:, q * 4:(q + 1) * 4, :], wfv[:, q * 4:(q + 1) * 4, :], writes=[wfm_b])
    pg.dma("sync", wtm[:], wtm_ap.rearrange("(k p) f -> p k f", p=128), writes=[wtm_b])
    bmt, bmt_b = sbt("bmt", [128, 12, 256], BF16)
    pg.dma("gpsimd", bmt[:], bmt_ap.rearrange("b h v k q -> k (b h v) q"), writes=[bmt_b])
    cst, cst_b = sbt("cst", [128, 4], F32)
    pg.dma("sync", cst[:], cst_ap, writes=[cst_b])
    hgb, hgb_b = sbt("hgb", [128, 2], F32)
    pg.dma("sync", hgb[:], hgb_ap.rearrange("l p -> p l"), writes=[hgb_b], allow_slow_non_contiguous=True)
    lbv, lb_b = sbt("lbv", [128, 3], F32)
    if layer == 0:
        pg.op("vector", lambda e: e.memset(lbv[:, 0:1], 0.0), writes=[lb_b])
    else:
        pg.op("vector", lambda e: e.tensor_tensor(out=lbv[:, 0:1], in0=hgb[:, 1:2], in1=hgb[:, 0:1],
                                                  op=ALU.subtract), reads=[hgb_b], writes=[lb_b])
        pg.op("scalar", lambda e: e.activation(out=lbv[:, 0:1], in_=lbv[:, 0:1], func=AF.Sigmoid),
              reads=[lb_b], writes=[lb_b])
    pg.op("vector", lambda e: e.tensor_scalar(out=lbv[:, 1:2], in0=lbv[:, 0:1], scalar1=-1.0, scalar2=1.0,
                                              op0=ALU.mult, op1=ALU.add), reads=[lb_b], writes=[lb_b])
    pg.op("vector", lambda e: e.tensor_scalar(out=lbv[:, 2:3], in0=lbv[:, 0:1], scalar1=-1.0, scalar2=None,
                                              op0=ALU.add), reads=[lb_b], writes=[lb_b])

    bk = [pg.ps(tag + "_bk%d" % i, [128, 512], F32) for i in range(8)]
    bk_b = [Buf("bk%d" % i) for i in range(8)]
    bk3bf = bk[3][:].bitcast(BF16)

    hTb = [sbt("hT%d" % i, [128, KT, TB], BF16) for i in range(2)]
    F = {}
    for nm in ["qf", "sig", "kk", "bb", "d1", "E1", "E2", "gh", "gr", "rstd", "tmp"]:
        F[nm] = sbt(nm, [128, TB], F32)
    H = {}
    for nm in ["qt", "kt", "qin", "kout", "sq"]:
        H[nm] = sbt(nm, [128, TB], BF16)
    mix = [sbt("mix%d" % i, [128, TB], BF16) for i in range(2)]
    v_h, v_h_b = sbt("v_h", [128, 4, 128], BF16)
    v_r, v_r_b = sbt("v_r", [128, 4, 128], BF16)
    ATm, ATm_b = sbt("ATm", [128, 128], BF16)
    koT, koT_b = sbt("koT", [128, 128], BF16)
    S_h, S_h_b = sbt("S_h", [128, 128], F32)
    S_hb, S_hb_b = sbt("S_hb", [128, 128], BF16)
    ebl, ebl_b = sbt("ebl", [128, 4], F32)
    S_r, S_r_b = sbt("S_r", [64, 128], F32)
    S_rb, S_rb_b = sbt("S_rb", [64, 128], BF16)
    rt = [sbt("rt%d" % i, [128, 4, 256], F32) for i in range(1)]
    m1, m1_b = sbt("m1", [128, 128], F32)
    qks, qks_b = sbt("qks", [128, 128], F32)
    m2, m2_b = sbt("m2", [128, 128], F32)
    qkp, qkp_b = sbt("qkp", [128, 4, 128], BF16)
    qkT, qkT_b = sbt("qkT", [64, 1024], BF16)
    STm, STm_b = sbt("STm", [128, 4, 128], BF16)
    KTr, KTr_b = sbt("KTr", [128, 2, RING], BF16)
    VTr, VTr_b = sbt("VTr", [128, 2, RING], BF16)
    QT, QT_b = sbt("QT", [128, 2, SBS], BF16)
    acc, acc_b = sbt("acc", [128, 2, SBS], F32)
    Et = [sbt("Eatt%d" % i, [128, 256], BF16) for i in range(2)]
    Vs = [sbt("Vs%d" % i, [128, 256], BF16) for i in range(2)]
    negc, negc_b = sbt("negc", [1, 2, SBS], BF16)
    Rk2, Rk2_b = sbt("Rk2", [1, 2], F32)
    r11, r11_b = sbt("r11", [1, 1], F32)
    trow, trow_b = sbt("trow", [1, TB], F32)
    out_b = Buf("mixT_dram")

    pg.op("vector", lambda e: e.memset(ATm[:], 0.0), writes=[ATm_b])
    pg.op("vector", lambda e: e.memset(S_h[:], 0.0), writes=[S_h_b])
    pg.op("vector", lambda e: e.memset(S_hb[:], 0.0), writes=[S_hb_b])
    pg.op("vector", lambda e: e.memset(S_r[:], 0.0), writes=[S_r_b])
    pg.op("vector", lambda e: e.memset(S_rb[:], 0.0), writes=[S_rb_b])
    pg.op("vector", lambda e: e.memset(Rk2[:], 0.0), writes=[Rk2_b])

    def act(out, in_, func, reads, writes, **kw):
        pg.op("scalar", lambda e: e.activation(out=out, in_=in_, func=func, **kw), reads, writes)

    def tt(eng, out, in0, in1, op, reads, writes):
        pg.op(eng, lambda e: e.tensor_tensor(out=out, in0=in0, in1=in1, op=op), reads, writes)

    def ts(eng, out, in0, s1, s2, op0, op1, reads, writes):
        if op1 is None:
            pg.op(eng, lambda e: e.tensor_scalar(out=out, in0=in0, scalar1=s1, scalar2=None, op0=op0), reads, writes)
        else:
            pg.op(eng, lambda e: e.tensor_scalar(out=out, in0=in0, scalar1=s1, scalar2=s2, op0=op0, op1=op1),
                  reads, writes)

    def stt(out, in0, scalar, in1, op0, op1, reads, writes):
        pg.op("vector", lambda e: e.scalar_tensor_tensor(out=out, in0=in0, scalar=scalar, in1=in1, op0=op0, op1=op1),
              reads, writes)

    def mm(out, lhsT, rhs, start, stop, reads, writes):
        pg.op("tensor", lambda e: e.matmul(out, lhsT=lhsT, rhs=rhs, start=start, stop=stop), reads, writes)

    def tr(out, in_, reads, writes):
        pg.op("tensor", lambda e: e.transpose(out=out, in_=in_, identity=ident[:]), reads + [ident_b], writes)

    def cp(eng, out, in_, reads, writes):
        if eng == "scalar":
            pg.op("scalar", lambda e: e.copy(out=out, in_=in_), reads, writes)
        else:
            pg.op(eng, lambda e: e.tensor_copy(out=out, in_=in_), reads, writes)

    def post_norm_gate(obank, gate, gate_b, mixt, mixt_b, tile_idx, blk):
        sq, sq_b = H["sq"]
        rstd, rstd_b = F["rstd"]
        tmp, tmp_b = F["tmp"]
        act(sq[:], bk[obank][:], AF.Square, [bk_b[obank]], [sq_b])
        mm(bk[6][:], ones_bf[:], sq[:], True, True, [ones_b, sq_b], [bk_b[6]])
        act(rstd[:], bk[6][:], AF.Ln, [bk_b[6], eps_b], [rstd_b], scale=1.0 / 128.0, bias=epsb[:])
        act(rstd[:], rstd[:], AF.Exp, [rstd_b], [rstd_b], scale=-0.5)
        tt("vector", tmp[:], bk[obank][:], rstd[:], ALU.mult, [bk_b[obank], rstd_b], [tmp_b])
        tt(GP, mixt[:], tmp[:], gate[:], ALU.mult, [tmp_b, gate_b], [mixt_b])
        pg.dma("sync", mixT_ap[tile_idx, :, blk * TB:(blk + 1) * TB], mixt[:], reads=[mixt_b], writes=[out_b])

    mixc = [0]

    def next_mix():
        m = mix[mixc[0] % 2]
        mixc[0] += 1
        return m

    def load_block(b):
        t, tb_ = hTb[b % 2]
        pg.dma("sync", t[:], hT_ap[:, :, b * TB:(b + 1) * TB].rearrange("k p t -> p k t"), writes=[tb_])

    def load_rt(b):
        r, rb = rt[0]
        pg.dma("sync", r[:], rtab_ap[b * TB:(b + 1) * TB, :].rearrange("(i p) c -> p i c", p=128), writes=[rb])

    def ring_idx(pos_abs):
        return pos_abs % RING

    def strided(t3, h, start, d):
        return t3[:, h, start:start + 127 * d + 1:d]

    att_cnt = [0]

    def att_block(h, br, d, start_abs, B0, first):
        n = att_cnt[0]
        att_cnt[0] += 1
        sbank = 2 if n % 2 == 0 else 4
        obank = 5 if n % 2 == 0 else 6
        E, E_b = Et[n % 2]
        V, V_b = Vs[n % 2]
        prev_abs = start_abs - 128 * d
        var = 1
        if prev_abs < 0:
            prev_abs = start_abs
            var = 0
        ks = [ring_idx(prev_abs), ring_idx(start_abs)]
        qs = start_abs - B0
        tab = bmt[:, (br * 2 + h) * 2 + var, :]
        for kb in range(2):
            tr(bk3bf[:, kb * 128:(kb + 1) * 128], strided(VTr, h, ks[kb], d), [VTr_b], [bk_b[3]])
        cp("scalar", V[:], bk3bf[:, 0:256], [bk_b[3]], [V_b])
        for kb in range(2):
            o = bk[sbank][:, kb * 128:(kb + 1) * 128]
            mm(o, strided(KTr, h, ks[kb], d), strided(QT, h, qs, d), True, False, [KTr_b, QT_b], [bk_b[sbank]])
            mm(o, ones_bf[0:1, :], negc[0:1, h, qs:qs + 127 * d + 1:d], False, False, [ones_b, negc_b],
               [bk_b[sbank]])
            mm(o, ident[:], tab[:, kb * 128:(kb + 1) * 128], False, True, [ident_b, bmt_b], [bk_b[sbank]])
        act(E[:], bk[sbank][:, 0:256], AF.Exp, [bk_b[sbank]], [E_b])
        for kb in range(2):
            mm(bk[obank][:, 0:128], V[:, kb * 128:(kb + 1) * 128], E[:, kb * 128:(kb + 1) * 128],
               kb == 0, kb == 1, [V_b, E_b], [bk_b[obank]])
        for kb in range(2):
            mm(bk[obank][:, 128:256], ones_bf[:], E[:, kb * 128:(kb + 1) * 128],
               kb == 0, kb == 1, [ones_b, E_b], [bk_b[obank]])
        a_sl = acc[:, :, qs:qs + 127 * d + 1:d]
        o_v = bk[obank][:, 0:256].rearrange("p (a q) -> p a q", a=2)
        if first:
            cp("vector", a_sl, o_v, [bk_b[obank]], [acc_b])
        else:
            tt("vector", a_sl, a_sl, o_v, ALU.add, [bk_b[obank], acc_b], [acc_b])

    def att_superblock(sb):
        B0 = sb * SBS
        for h in range(2):
            for br, d in enumerate([1, 4, 16]):
                span = 128 * d
                for g in range(SBS // span):
                    for r in range(d):
                        att_block(h, br, d, B0 + g * span + r, B0, br == 0)
            pg.op("vector", lambda e: e.reciprocal(out=acc[:, 1, :], in_=acc[:, 1, :]), [acc_b], [acc_b])
            for q in range(SBS // TB):
                mt, mt_b = next_mix()
                tt("vector", mt[:], acc[:, 0, q * TB:(q + 1) * TB], acc[:, 1, q * TB:(q + 1) * TB], ALU.mult,
                   [acc_b], [mt_b])
                pg.dma("sync", mixT_ap[2 + h, :, B0 + q * TB:B0 + (q + 1) * TB], mt[:], reads=[mt_b],
                       writes=[out_b])

    load_block(0)
    load_rt(0)
    for b in range(nblk):
        if b + 1 < nblk:
            load_block(b + 1)
        hT, hT_b = hTb[b % 2]
        rtt, rtt_b = rt[0]
        sbpos = (b * TB) % SBS
        rpos = (b * TB) % RING
        for c in range(NFM):
            pb = c % 2
            for k in range(KT):
                mm(bk[pb][:], wfm[:, k, c * 128:(c + 1) * 128], hT[:, k, :], k == 0, k == KT - 1,
                   [wfm_b, hT_b], [bk_b[pb]])
            src, srcb = bk[pb][:], [bk_b[pb]]
            if c == 0:
                act(F["qf"][0][:], src, AF.Silu, srcb, [F["qf"][1]])
            elif c == 1:
                act(F["sig"][0][:], src, AF.Sigmoid, srcb, [F["sig"][1]])
            elif c == 2:
                act(F["gh"][0][:], src, AF.Silu, srcb, [F["gh"][1]])
            elif c == 3:
                act(F["gr"][0][:], src, AF.Silu, srcb, [F["gr"][1]])
            else:
                h = (c - 4) // 3
                kind = (c - 4) % 3
                if kind == 0:
                    act(QT[:, h, sbpos:sbpos + TB], src, AF.Copy, srcb, [QT_b], scale=float(128 ** -0.5))
                    sqs = QT[:, h, sbpos:sbpos + TB]
                    sqb = QT_b
                elif kind == 1:
                    cp("scalar", KTr[:, h, rpos:rpos + TB], src, srcb, [KTr_b])
                    sqs = KTr[:, h, rpos:rpos + TB]
                    sqb = KTr_b
                else:
                    cp("scalar", VTr[:, h, rpos:rpos + TB], src, srcb, [VTr_b])
                    continue
                if kind == 1 and 'stab' in stages:
                    sq, sq_b = H["sq"]
                    tt(GP, sq[:], sqs, sqs, ALU.mult, [sqb], [sq_b])
                    mm(bk[7][0:1, :], ones_bf[:, 0:1], sq[:], True, True, [ones_b, sq_b], [bk_b[7]])
                    pg.op("vector", lambda e: e.reduce_max(out=r11[:], in_=bk[7][0:1, :], axis=AX.X),
                          [bk_b[7]], [r11_b])
                    tt("vector", Rk2[0:1, h:h + 1], Rk2[0:1, h:h + 1], r11[:], ALU.max, [Rk2_b, r11_b], [Rk2_b])
        for h in (range(2) if 'stab' in stages else []):
            sq, sq_b = H["sq"]
            qs_ = QT[:, h, sbpos:sbpos + TB]
            tt(GP, sq[:], qs_, qs_, ALU.mult, [QT_b], [sq_b])
            mm(bk[7][0:1, :], ones_bf[:, 0:1], sq[:], True, True, [ones_b, sq_b], [bk_b[7]])
            ts("vector", trow[:], bk[7][0:1, :], Rk2[0:1, h:h + 1], 1e-30, ALU.mult, ALU.add,
               [bk_b[7], Rk2_b], [trow_b])
            act(trow[:], trow[:], AF.Ln, [trow_b], [trow_b])
            act(trow[:], trow[:], AF.Exp, [trow_b], [trow_b], scale=0.5)
            ts("vector", negc[0:1, h, sbpos:sbpos + TB], trow[:], -1.02, None, ALU.mult, None, [trow_b], [negc_b])

        if 'hgrn' in stages:
            qf, sig, kk, bb, d1, E1, E2 = [F[n][0] for n in ["qf", "sig", "kk", "bb", "d1", "E1", "E2"]]
            qf_b, sig_b, kk_b, bb_b, d1_b, E1_b, E2_b = [F[n][1] for n in ["qf", "sig", "kk", "bb", "d1", "E1", "E2"]]
            lf, lf_b = sig, sig_b
            ts(GP, kk[:], sig[:], -1.0, lbv[:, 2:3], ALU.add, ALU.mult, [sig_b, lb_b], [kk_b])
            ts("vector", lf[:], sig[:], lbv[:, 1:2], lbv[:, 0:1], ALU.mult, ALU.add, [sig_b, lb_b], [lf_b])
            act(lf[:], lf[:], AF.Ln, [lf_b], [lf_b])
            pg.op("vector", lambda e: e.tensor_tensor_scan(out=bb[:], data0=segm[:], data1=lf[:], initial=0.0,
                                                           op0=ALU.mult, op1=ALU.add),
                  [segm_b, lf_b], [bb_b])
            bb3 = bb[:].rearrange("p (a t) -> p a t", a=4)
            d13 = d1[:].rearrange("p (a t) -> p a t", a=4)
            tt("vector", d13, bb3, bb3[:, :, 63:64].broadcast_to([128, 4, 128]), ALU.subtract, [bb_b], [d1_b])
            ts(GP, d1[:], d1[:], -80.0, 80.0, ALU.max, ALU.min, [d1_b], [d1_b])
            act(E1[:], d1[:], AF.Exp, [d1_b], [E1_b])
            act(E2[:], d1[:], AF.Exp, [d1_b], [E2_b], scale=-1.0)
            tt("vector", H["qt"][0][:], qf[:], E1[:], ALU.mult, [qf_b, E1_b], [H["qt"][1]])
            tt(GP, H["kt"][0][:], kk[:], E2[:], ALU.mult, [kk_b, E2_b], [H["kt"][1]])
            act(E1[:], bb[:], AF.Exp, [bb_b, H["qt"][1]], [E1_b])
            tt("vector", H["qin"][0][:], qf[:], E1[:], ALU.mult, [qf_b, E1_b], [H["qin"][1]])
            tt("vector", d13, bb3[:, :, 127:128].broadcast_to([128, 4, 128]), bb3, ALU.subtract, [bb_b, E2_b], [d1_b])
            act(E2[:], d1[:], AF.Exp, [d1_b, H["kt"][1]], [E2_b])
            tt(GP, H["kout"][0][:], kk[:], E2[:], ALU.mult, [kk_b, E2_b], [H["kout"][1]])
            act(ebl[:], bb3[:, :, 127], AF.Exp, [bb_b], [ebl_b])

        if 'tm' in stages:
            for i in range(4):
                for k in range(KT):
                    mm(bk[7][:, 0:NTM], hT[:, k, i * 128:(i + 1) * 128], wtm[:, k, :], k == 0, k == KT - 1,
                       [hT_b, wtm_b], [bk_b[7]])
                cp("scalar", v_h[:, i, :], bk[7][:, 0:128], [bk_b[7]], [v_h_b])
                cp("scalar", v_r[:, i, :], bk[7][:, 256:384], [bk_b[7]], [v_r_b])
                if TMLEVEL < 2:
                    continue
                cp("scalar", qks[:], bk[7][:, 128:256], [bk_b[7]], [qks_b])
                qk4 = qks[:].rearrange("p (a h c) -> p a h c", a=2, h=2)
                s4 = rtt[:, i, 128:256].rearrange("p (a h c) -> p a h c", a=2, h=2)
                m24 = m2[:].rearrange("p (a h c) -> p a h c", a=2, h=2)
                tt("vector", m1[:], qks[:], rtt[:, i, 0:128], ALU.mult, [qks_b, rtt_b], [m1_b])
                tt("vector", m24[:, :, 0, :], qk4[:, :, 1, :], s4[:, :, 0, :], ALU.mult, [qks_b, rtt_b], [m2_b])
                tt("vector", m24[:, :, 1, :], qk4[:, :, 0, :], s4[:, :, 1, :], ALU.mult, [qks_b, rtt_b], [m2_b])
                tt(GP, qkp[:, i, :], m1[:], m2[:], ALU.add, [m1_b, m2_b], [qkp_b])
                if TMLEVEL < 3:
                    continue
                tr(bk3bf[0:64, (2 * i) * 128:(2 * i + 1) * 128], qkp[:, i, 0:64], [qkp_b], [bk_b[3]])
                tr(bk3bf[0:64, (2 * i + 1) * 128:(2 * i + 2) * 128], qkp[:, i, 64:128], [qkp_b], [bk_b[3]])
            if TMLEVEL >= 3:
                cp("scalar", qkT[:], bk3bf[0:64, 0:1024], [bk_b[3]], [qkT_b])
            if b + 1 < nblk:
                load_rt(b + 1)

        if 'hgrn' in stages:
            qt, kt, qin, kout = [H[n][0] for n in ["qt", "kt", "qin", "kout"]]
            qt_b, kt_b, qin_b, kout_b = [H[n][1] for n in ["qt", "kt", "qin", "kout"]]
            for i in range(4):
                sl = slice(i * 128, (i + 1) * 128)
                mm(bk[2][:, 0:128], kt[:, sl], qt[:, sl], True, True, [kt_b, qt_b], [bk_b[2]])
                pg.op("vector", lambda e: e.copy_predicated(out=ATm[:], mask=cmask[:].bitcast(mybir.dt.uint32),
                                                            data=bk[2][:, 0:128]),
                      [bk_b[2], cmask_b, ATm_b], [ATm_b])
                tr(bk3bf[:, 0:128], kout[:, sl], [kout_b], [bk_b[3]])
                cp("scalar", koT[:], bk3bf[:, 0:128], [bk_b[3]], [koT_b])
                mm(bk[5][:, sl], v_h[:, i, :], ATm[:], True, False, [v_h_b, ATm_b], [bk_b[5]])
                mm(bk[5][:, sl], S_hb[:], qin[:, sl], False, True, [S_hb_b, qin_b], [bk_b[5]])
                mm(bk[4][:, 0:128], koT[:], v_h[:, i, :], True, True, [koT_b, v_h_b], [bk_b[4]])
                stt(S_h[:], S_h[:], ebl[:, i:i + 1], bk[4][:, 0:128], ALU.mult, ALU.add, [S_h_b, ebl_b, bk_b[4]], [S_h_b])
                cp(GP, S_hb[:], S_h[:], [S_h_b], [S_hb_b])
            mt, mt_b = next_mix()
            post_norm_gate(5, F["gh"][0], F["gh"][1], mt, mt_b, 0, b)

        if 'ret' in stages:
            for i in range(4):
                mm(bk[2][:, i * 128:(i + 1) * 128], qkT[:, (2 * i + 1) * 128:(2 * i + 2) * 128],
                   qkT[:, (2 * i) * 128:(2 * i + 1) * 128], True, True, [qkT_b], [bk_b[2]])
            tt("vector", STm[:], bk[2][:].rearrange("p (a t) -> p a t", a=4),
               cmask[:].rearrange("p (a t) -> p a t", a=1).broadcast_to([128, 4, 128]), ALU.mult,
               [bk_b[2], cmask_b], [STm_b])
            for i in range(4):
                sl = slice(i * 128, (i + 1) * 128)
                mm(bk[5][:, sl], v_r[:, i, :], STm[:, i, :], True, False, [v_r_b, STm_b], [bk_b[5]])
                mm(bk[5][:, sl], S_rb[:], qkT[:, (2 * i) * 128:(2 * i + 1) * 128], False, True, [S_rb_b, qkT_b],
                   [bk_b[5]])
                mm(bk[4][0:64, 0:128], qkp[:, i, 64:128], v_r[:, i, :], True, True, [qkp_b, v_r_b], [bk_b[4]])
                ts(GP, S_r[:], S_r[:], cst[0:64, 0:1], None, ALU.mult, None, [S_r_b, cst_b], [S_r_b])
                stt(S_r[:], bk[4][0:64, 0:128], cst[0:64, 0:1], S_r[:], ALU.mult, ALU.add, [bk_b[4], cst_b, S_r_b],
                    [S_r_b])
                cp(GP, S_rb[:], S_r[:], [S_r_b], [S_rb_b])
            mt, mt_b = next_mix()
            post_norm_gate(5, F["gr"][0], F["gr"][1], mt, mt_b, 1, b)

        if (b * TB + TB) % SBS == 0 and 'att' in stages:
            att_superblock((b * TB) // SBS)
    return out_b


OFF = dict(hq=0, hf=512, hi=1024, hg=1536, rq=2048, rk=2304, rv=2560, rg=3072, aq=3584, ak=4608, av=5632)


def wfm_cols(g):
    cols = []
    for nm in ["hq", "hf", "hg"]:
        cols += list(range(OFF[nm] + g * 128, OFF[nm] + (g + 1) * 128))
    cols += list(range(OFF["rg"] + g * 128, OFF["rg"] + (g + 1) * 128))
    for h in (2 * g, 2 * g + 1):
        for nm in ["aq", "ak", "av"]:
            cols += list(range(OFF[nm] + h * 128, OFF[nm] + (h + 1) * 128))
    return np.array(cols)


def wtm_cols(g):
    cols = list(range(OFF["hi"] + g * 128, OFF["hi"] + (g + 1) * 128))
    cols += list(range(OFF["rq"] + g * 64, OFF["rq"] + (g + 1) * 64))
    cols += list(range(OFF["rk"] + g * 64, OFF["rk"] + (g + 1) * 64))
    cols += list(range(OFF["rv"] + g * 128, OFF["rv"] + (g + 1) * 128))
    return np.array(cols)


def rope_tables(S, g):
    inv_freq = (np.float32(10000.0) ** (-np.arange(0, 64, 2, dtype=np.float32) / np.float32(64))).astype(np.float32)
    ang = (np.arange(S, dtype=np.float32)[:, None] * inv_freq[None, :]).astype(np.float32)
    c = np.cos(ang.astype(np.float64))
    s = np.sin(ang.astype(np.float64))
    gamma = 1.0 - 2.0 ** (-5.0 - g)
    i = (np.arange(S) % 128).astype(np.float64)
    gq = (gamma ** (i + 1))[:, None]
    gk = (gamma ** (-(i + 1)) / 8.0)[:, None]
    C4 = np.concatenate([c * gq, c * gq, c * gk, c * gk], axis=1)
    S4 = np.concatenate([-s * gq, s * gq, -s * gk, s * gk], axis=1)
    return np.concatenate([C4, S4], axis=1).astype(np.float32)


def t5_bucket_np(distance):
    max_exact = 16
    n = np.maximum(distance, 1).astype(np.float32)
    large = max_exact + (np.log(n / np.float32(max_exact)) / np.float32(math.log(2048 / max_exact))
                         * np.float32(32 - max_exact)).astype(np.int32)
    large = np.minimum(large, 31)
    return np.where(distance < max_exact, distance, large)


def bias_tables(rel_bias, g, neg=-30000.0):
    out = np.full((3, 2, 2, 128, 256), neg, np.float32)
    k = np.arange(128)[:, None, None]
    kb = np.arange(2)[None, :, None]
    a = np.arange(128)[None, None, :]
    kk = kb * 128 + k
    j = a + 128 - kk
    valid = (j >= 0) & (j <= 128)
    for br, d in enumerate([1, 4, 16]):
        bucket = t5_bucket_np(np.clip(j, 0, 128) * d)
        for h in range(2):
            vals = rel_bias[bucket, 2 * g + h]
            for var in range(2):
                v = valid & ((kb == 1) | (var == 1))
                t = np.where(v, vals, np.float32(neg)).astype(np.float32)
                out[br, h, var] = t.reshape(128, 256)
    return out


def consts(g):
    gamma = 1.0 - 2.0 ** (-5.0 - g)
    c = np.zeros((128, 4), np.float32)
    c[:, 0] = gamma ** 128
    return c


def phase_F(pg, x_ap, g_ap, y_ap, T, tag="F"):
    gain = pg.sb(tag + "_gain", [128, D], F32)
    gain_b = Buf()
    epsb = pg.sb(tag + "_eps", [128, 1], F32)
    eps_b = Buf()
    junk = pg.sb(tag + "_junk", [128, D], BF16)
    junk_b = Buf()
    xt = [pg.sb(tag + "_xt%d" % i, [128, D], F32) for i in range(2)]
    xt_b = [Buf() for i in range(2)]
    yt = [pg.sb(tag + "_yt%d" % i, [128, D], F32) for i in range(2)]
    yt_b = [Buf() for i in range(2)]
    ss = [pg.sb(tag + "_ss%d" % i, [128, 1], F32) for i in range(2)]
    ss_b = [Buf() for i in range(2)]
    out_b = Buf()
    pg.dma("sync", gain[:], g_ap.partition_broadcast(128), writes=[gain_b])
    pg.op("vector", lambda e: e.memset(epsb[:], EPS), writes=[eps_b])
    for i in range(T // 128):
        p = i % 2
        pg.dma("sync", xt[p][:], x_ap[i * 128:(i + 1) * 128, :], writes=[xt_b[p]])
        pg.op("scalar", lambda e, p=p: e.activation(out=junk[:], in_=xt[p][:], func=AF.Square, accum_out=ss[p][:]),
              reads=[xt_b[p]], writes=[junk_b, ss_b[p]])
        pg.op("scalar", lambda e, p=p: e.activation(out=ss[p][:], in_=ss[p][:], func=AF.Ln, scale=1.0 / D,
                                                     bias=epsb[:]), reads=[ss_b[p], eps_b], writes=[ss_b[p]])
        pg.op("scalar", lambda e, p=p: e.activation(out=ss[p][:], in_=ss[p][:], func=AF.Exp, scale=-0.5),
              reads=[ss_b[p]], writes=[ss_b[p]])
        pg.op("vector", lambda e, p=p: e.scalar_tensor_tensor(out=yt[p][:], in0=xt[p][:], scalar=ss[p][:],
                                                              in1=gain[:], op0=ALU.mult, op1=ALU.mult),
              reads=[xt_b[p], ss_b[p], gain_b], writes=[yt_b[p]])
        pg.dma("sync", y_ap[i * 128:(i + 1) * 128, :], yt[p][:], reads=[yt_b[p]], writes=[out_b])
    return out_b


BATCH = 2
SEQ = 16384
NCORE = 8
TSEG = SEQ // 4
WNAMES = [("w_in", (D, 6656)), ("w_out", (D, D)), ("w_gate", (D, FF)), ("w_up", (D, FF)), ("w_down", (FF, D))]
_bf = ml_dtypes.bfloat16


def _new():
    return bass.Bass("TRN2", target_bir_lowering=False)


def _run(nc, ims):
    res = run_bass_kernel_spmd(nc, ims, core_ids=list(range(NCORE)))
    return res.results


def _build_W():
    nc = _new()
    pairs = []
    for l in range(2):
        for n, shp in WNAMES:
            r = shp[0] // NCORE
            s = nc.dram_tensor("%s%d" % (n, l), [r, shp[1]], F32, kind="ExternalInput").ap()
            d = nc.dram_tensor("%s%d_bf" % (n, l), [r, shp[1]], BF16, kind="ExternalOutput").ap()
            pairs.append((s, d))
    pg = Prog(nc)
    phase_W(pg, pairs)
    pg.emit()
    return nc


def _build_A(T):
    nc = _new()
    x = nc.dram_tensor("x", [T, D], F32, kind="ExternalInput").ap()
    g = nc.dram_tensor("g", [D], F32, kind="ExternalInput").ap()
    hT = nc.dram_tensor("hT", [KT, 128, T], BF16, kind="ExternalOutput").ap()
    pg = Prog(nc)
    phase_A(pg, x, g, hT, T)
    pg.emit()
    return nc


def _build_B(S, layer):
    nc = _new()
    hT = nc.dram_tensor("hT", [KT, 128, S], BF16, kind="ExternalInput").ap()
    wfm = nc.dram_tensor("wfm", [D, NFM * 128], BF16, kind="ExternalInput").ap()
    wtm = nc.dram_tensor("wtm", [D, NTM], BF16, kind="ExternalInput").ap()
    rtab = nc.dram_tensor("rtab", [S, 256], F32, kind="ExternalInput").ap()
    bmt = nc.dram_tensor("bmt", [3, 2, 2, 128, 256], F32, kind="ExternalInput").ap()
    hgb = nc.dram_tensor("hgb", [2, 128], F32, kind="ExternalInput").ap()
    cst = nc.dram_tensor("cst", [128, 4], F32, kind="ExternalInput").ap()
    mixT = nc.dram_tensor("mixT", [4, 128, S], BF16, kind="ExternalOutput").ap()
    pg = Prog(nc)
    phase_B(pg, hT, wfm, wtm, rtab, bmt, hgb, cst, mixT, S, layer)
    pg.emit()
    return nc


def _build_C1(T):
    nc = _new()
    mT = nc.dram_tensor("mT", [KT, 128, T], BF16, kind="ExternalInput").ap()
    x = nc.dram_tensor("x", [T, D], F32, kind="ExternalInput").ap()
    wo = nc.dram_tensor("wo", [D, D], BF16, kind="ExternalInput").ap()
    g = nc.dram_tensor("g", [D], F32, kind="ExternalInput").ap()
    x1 = nc.dram_tensor("x1", [T, D], F32, kind="ExternalOutput").ap()
    uT = nc.dram_tensor("uT", [KT, 128, T], BF16, kind="ExternalOutput").ap()
    pg = Prog(nc)
    phase_C1(pg, mT, x, wo, g, x1, uT, T)
    pg.emit()
    return nc


def _build_C2(T):
    nc = _new()
    uT = nc.dram_tensor("uT", [KT, 128, T], BF16, kind="ExternalInput").ap()
    uTh = nc.dram_tensor("uTh", [KT, 128, 2], BF16, kind="ExternalInput").ap()
    x1 = nc.dram_tensor("x1", [T, D], F32, kind="ExternalInput").ap()
    wg = nc.dram_tensor("wg", [D, FF], BF16, kind="ExternalInput").ap()
    wu = nc.dram_tensor("wu", [D, FF], BF16, kind="ExternalInput").ap()
    wd = nc.dram_tensor("wd", [FF, D], BF16, kind="ExternalInput").ap()
    cw = nc.dram_tensor("cw", [3, FF], F32, kind="ExternalInput").ap()
    cb = nc.dram_tensor("cb", [FF], F32, kind="ExternalInput").ap()
    x2 = nc.dram_tensor("x2", [T, D], F32, kind="ExternalOutput").ap()
    pg = Prog(nc)
    phase_C2(pg, uT, uTh, x1, wg, wu, wd, cw, cb, x2, T)
    pg.emit()
    return nc


def _build_F(T):
    nc = _new()
    x = nc.dram_tensor("x", [T, D], F32, kind="ExternalInput").ap()
    g = nc.dram_tensor("g", [D], F32, kind="ExternalInput").ap()
    y = nc.dram_tensor("y", [T, D], F32, kind="ExternalOutput").ap()
    pg = Prog(nc)
    phase_F(pg, x, g, y, T)
    pg.emit()
    return nc


def kernel(x, norm_mix, w_in, hg_lower_bound, w_out, norm_ffn, w_gate, conv_w, conv_b, w_up, w_down, rel_bias,
           norm_final):
    f32 = np.float32
    x = np.asarray(x, f32)
    S, T = SEQ, TSEG
    W = dict(w_in=np.asarray(w_in, f32), w_out=np.asarray(w_out, f32), w_gate=np.asarray(w_gate, f32),
             w_up=np.asarray(w_up, f32), w_down=np.asarray(w_down, f32))
    hgl = np.asarray(hg_lower_bound, f32)
    relb = np.asarray(rel_bias, f32)
    ncW = _build_W()
    ims = []
    for c in range(NCORE):
        m = {}
        for l in range(2):
            for n, shp in WNAMES:
                r = shp[0] // NCORE
                m["%s%d" % (n, l)] = np.ascontiguousarray(W[n][l][c * r:(c + 1) * r])
        ims.append(m)
    res = _run(ncW, ims)
    wbf = {}
    for l in range(2):
        for n, shp in WNAMES:
            wbf[(n, l)] = np.concatenate([np.asarray(res[c]["%s%d_bf" % (n, l)]) for c in range(NCORE)], axis=0)
    del res, ims
    ncA = _build_A(T)
    ncC1 = _build_C1(T)
    ncC2 = _build_C2(T)
    xs = [np.ascontiguousarray(x[c // 4, (c % 4) * T:(c % 4 + 1) * T]) for c in range(NCORE)]
    rtabs = [rope_tables(S, g) for g in range(4)]
    bmts = [bias_tables(relb, g) for g in range(4)]
    for l in range(2):
        g_mix = np.ascontiguousarray(np.asarray(norm_mix, f32)[l])
        res = _run(ncA, [{"x": xs[c], "g": g_mix} for c in range(NCORE)])
        hT_all = [np.concatenate([np.asarray(res[b * 4 + s]["hT"]) for s in range(4)], axis=2) for b in range(2)]
        ncB = _build_B(S, l)
        win = wbf[("w_in", l)]
        ims = []
        for c in range(NCORE):
            b, g = c // 4, c % 4
            ims.append({"hT": hT_all[b], "wfm": np.ascontiguousarray(win[:, wfm_cols(g)]),
                        "wtm": np.ascontiguousarray(win[:, wtm_cols(g)]), "rtab": rtabs[g], "bmt": bmts[g],
                        "hgb": np.ascontiguousarray(hgl[:, g * 128:(g + 1) * 128]), "cst": consts(g)})
        res = _run(ncB, ims)
        mixT_all = []
        for b in range(2):
            m = np.zeros((KT, 128, S), _bf)
            for g in range(4):
                o = np.asarray(res[b * 4 + g]["mixT"])
                m[g] = o[0]
                m[4 + g] = o[1]
                m[8 + 2 * g] = o[2]
                m[8 + 2 * g + 1] = o[3]
            mixT_all.append(m)
        del res, ims, hT_all
        g_ffn = np.ascontiguousarray(np.asarray(norm_ffn, f32)[l])
        ims = []
        for c in range(NCORE):
            b, s = c // 4, c % 4
            ims.append({"mT": np.ascontiguousarray(mixT_all[b][:, :, s * T:(s + 1) * T]), "x": xs[c],
                        "wo": wbf[("w_out", l)], "g": g_ffn})
        res = _run(ncC1, ims)
        x1s = [np.asarray(res[c]["x1"]) for c in range(NCORE)]
        uTs = [np.asarray(res[c]["uT"]) for c in range(NCORE)]
        del res, ims, mixT_all
        ims = []
        for c in range(NCORE):
            s = c % 4
            halo = np.zeros((KT, 128, 2), _bf) if s == 0 else np.ascontiguousarray(uTs[c - 1][:, :, T - 2:T])
            ims.append({"uT": uTs[c], "uTh": halo, "x1": x1s[c], "wg": wbf[("w_gate", l)], "wu": wbf[("w_up", l)],
                        "wd": wbf[("w_down", l)], "cw": np.ascontiguousarray(np.asarray(conv_w, f32)[l]),
                        "cb": np.ascontiguousarray(np.asarray(conv_b, f32)[l])})
        res = _run(ncC2, ims)
        xs = [np.asarray(res[c]["x2"]) for c in range(NCORE)]
        del res, ims, x1s, uTs
    ncF = _build_F(T)
    res = _run(ncF, [{"x": xs[c], "g": np.ascontiguousarray(np.asarray(norm_final, f32))} for c in range(NCORE)])
    out = np.zeros((BATCH, SEQ, D), f32)
    for c in range(NCORE):
        out[c // 4, (c % 4) * T:(c % 4 + 1) * T] = np.asarray(res[c]["y"])
    return out
```
